# Optimizing a Trainium2 kernel written in Bass

```python
import math
import jax, jax.numpy as jnp
from jax import lax
import numpy as np

D_MODEL = 1024
BATCH = 8
SEQ = 4096
DEPTH = 1

CONV_CH = D_MODEL
CONV_K = 31
RET_HEADS = D_MODEL // 256
RET_DK = 256
RET_DV = 2 * RET_DK
RET_CHUNK = 128
ROPE_BASE = 10000.0
FFN_DIM = 3 * D_MODEL
FFN_CONV_K = 3
EPS = 1e-6

W_GLU = 2 * CONV_CH
W_Q = RET_HEADS * RET_DK
W_K = RET_HEADS * RET_DK
W_V = RET_HEADS * RET_DV
W_G = RET_HEADS * RET_DV
W_GATES = 2 * D_MODEL
N_IN = W_GLU + W_Q + W_K + W_V + W_G + W_GATES
SPLITS = [W_GLU, W_GLU + W_Q, W_GLU + W_Q + W_K, W_GLU + W_Q + W_K + W_V,
          W_GLU + W_Q + W_K + W_V + W_G]

kernel_name = "hybrid_conformer_conv_retention_gated_block"


def rms_norm(x, g):
    xf = x.astype(jnp.float32)
    y = xf * lax.rsqrt(jnp.mean(xf * xf, axis=-1, keepdims=True) + EPS)
    return (y * g.astype(jnp.float32)).astype(x.dtype)


def layer_norm(x, g, b):
    xf = x.astype(jnp.float32)
    mu = jnp.mean(xf, axis=-1, keepdims=True)
    var = jnp.mean(jnp.square(xf - mu), axis=-1, keepdims=True)
    y = (xf - mu) * lax.rsqrt(var + EPS)
    return (y * g.astype(jnp.float32) + b.astype(jnp.float32)).astype(x.dtype)


def causal_depthwise_conv(x, w, b):
    K, C = w.shape
    y = lax.conv_general_dilated(
        x, w[:, None, :].astype(x.dtype), window_strides=(1,),
        padding=[(K - 1, 0)], dimension_numbers=("NWC", "WIO", "NWC"),
        feature_group_count=C)
    return y + b.astype(x.dtype)


def rotary(x, pos):
    half = x.shape[-1] // 2
    inv_freq = ROPE_BASE ** (-jnp.arange(half, dtype=jnp.float32) / half)
    ang = pos[:, None] * inv_freq[None, :]
    cos = jnp.cos(ang)[None, :, None, :].astype(x.dtype)
    sin = jnp.sin(ang)[None, :, None, :].astype(x.dtype)
    x1, x2 = x[..., :half], x[..., half:]
    return jnp.concatenate([x1 * cos - x2 * sin, x2 * cos + x1 * sin], axis=-1)


def retention_chunkwise(q, k, v):
    B, S, H, dk = q.shape
    dv = v.shape[-1]
    C = RET_CHUNK
    N = S // C
    log_gamma = jnp.log1p(-jnp.power(2.0, -5.0 - jnp.arange(H, dtype=jnp.float32)))
    idx = jnp.arange(C, dtype=jnp.float32)
    diff = idx[:, None] - idx[None, :]
    causal = diff >= 0
    inner_decay = jnp.where(causal[None],
                            jnp.exp(jnp.where(causal, diff, 0.0)[None] * log_gamma[:, None, None]),
                            0.0)
    cross_decay = jnp.exp((idx + 1.0)[None, :] * log_gamma[:, None])
    state_decay = jnp.exp((C - 1.0 - idx)[None, :] * log_gamma[:, None])
    chunk_decay = jnp.exp(C * log_gamma)

    def to_chunks(t):
        return t.reshape(B, N, C, H, t.shape[-1]).transpose(1, 0, 3, 2, 4)

    def step(R, qkv):
        qc, kc, vc = qkv
        s = jnp.einsum("bhqd,bhkd->bhqk", qc, kc) * inner_decay[None]
        inner = jnp.einsum("bhqk,bhkv->bhqv", s, vc)
        cross = jnp.einsum("bhqd,bhdv->bhqv", qc, R) * cross_decay[None, :, :, None]
        R_new = R * chunk_decay[None, :, None, None] + jnp.einsum(
            "bhkd,bhkv->bhdv", kc, vc * state_decay[None, :, :, None])
        return R_new, inner + cross

    R0 = jnp.zeros((B, H, dk, dv), jnp.float32)
    _, out = lax.scan(step, R0, (to_chunks(q), to_chunks(k), to_chunks(v)))
    return out.transpose(1, 0, 3, 2, 4).reshape(B, S, H, dv)


def setup_inputs(seed: int = 0) -> dict:
    key = jax.random.key(seed)
    ks = jax.random.split(key, 24)
    f32 = jnp.float32
    L = DEPTH

    def nrm(k, shape, scale):
        return jax.random.normal(k, shape, f32) * scale

    return {
        "x": jax.random.normal(ks[0], (BATCH, SEQ, D_MODEL), f32),
        "norm_mix_g": 1.0 + nrm(ks[1], (L, D_MODEL), 0.02),
        "w_in": nrm(ks[2], (L, D_MODEL, N_IN), D_MODEL ** -0.5),
        "gate_b": nrm(ks[3], (L, W_GATES), 0.02),
        "conv_dw_w": nrm(ks[4], (L, CONV_K, CONV_CH), CONV_K ** -0.5),
        "conv_dw_b": nrm(ks[5], (L, CONV_CH), 0.02),
        "conv_ln_g": 1.0 + nrm(ks[6], (L, CONV_CH), 0.02),
        "conv_ln_b": nrm(ks[7], (L, CONV_CH), 0.02),
        "w_conv_proj": nrm(ks[8], (L, CONV_CH, D_MODEL), CONV_CH ** -0.5),
        "conv_proj_b": nrm(ks[9], (L, D_MODEL), 0.02),
        "ret_norm_g": 1.0 + nrm(ks[10], (L, RET_HEADS * RET_DV), 0.02),
        "w_ret_proj": nrm(ks[11], (L, RET_HEADS * RET_DV, D_MODEL), (RET_HEADS * RET_DV) ** -0.5),
        "w_out": nrm(ks[12], (L, D_MODEL, D_MODEL), D_MODEL ** -0.5),
        "norm_ffn_g": 1.0 + nrm(ks[13], (L, D_MODEL), 0.02),
        "w_up": nrm(ks[14], (L, D_MODEL, 2 * FFN_DIM), D_MODEL ** -0.5),
        "ffn_dw_w": nrm(ks[15], (L, FFN_CONV_K, 2 * FFN_DIM), FFN_CONV_K ** -0.5),
        "ffn_dw_b": nrm(ks[16], (L, 2 * FFN_DIM), 0.02),
        "w_down": nrm(ks[17], (L, FFN_DIM, D_MODEL), FFN_DIM ** -0.5),
        "norm_final_g": 1.0 + nrm(ks[18], (D_MODEL,), 0.02),
    }


def reference(x, norm_mix_g, w_in, gate_b, conv_dw_w, conv_dw_b, conv_ln_g, conv_ln_b,
              w_conv_proj, conv_proj_b, ret_norm_g, w_ret_proj, w_out, norm_ffn_g,
              w_up, ffn_dw_w, ffn_dw_b, w_down, norm_final_g):
    B, S, D = x.shape
    pos = jnp.arange(S, dtype=jnp.float32)
    for l in range(DEPTH):
        h = rms_norm(x, norm_mix_g[l])
        proj = h @ w_in[l].astype(x.dtype)
        glu_in, q, k, v, g, gate_logits = jnp.split(proj, SPLITS, axis=-1)

        a = glu_in[..., :CONV_CH] * jax.nn.sigmoid(glu_in[..., CONV_CH:])
        a = causal_depthwise_conv(a, conv_dw_w[l], conv_dw_b[l])
        a = jax.nn.silu(layer_norm(a, conv_ln_g[l], conv_ln_b[l]))
        y_a = a @ w_conv_proj[l].astype(x.dtype) + conv_proj_b[l].astype(x.dtype)

        q = rotary(q.reshape(B, S, RET_HEADS, RET_DK), pos) * (RET_DK ** -0.5)
        k = rotary(k.reshape(B, S, RET_HEADS, RET_DK), pos)
        v = v.reshape(B, S, RET_HEADS, RET_DV)
        r = retention_chunkwise(q.astype(jnp.float32), k.astype(jnp.float32),
                                v.astype(jnp.float32))
        mu = jnp.mean(r, axis=-1, keepdims=True)
        var = jnp.mean(jnp.square(r - mu), axis=-1, keepdims=True)
        r = ((r - mu) * lax.rsqrt(var + EPS)).reshape(B, S, RET_HEADS * RET_DV)
        r = (r * ret_norm_g[l].astype(jnp.float32)).astype(x.dtype)
        r = jax.nn.silu(g) * r
        y_b = r @ w_ret_proj[l].astype(x.dtype)

        gates = jax.nn.sigmoid(gate_logits + gate_b[l].astype(x.dtype))
        g_a, g_b = gates[..., :D_MODEL], gates[..., D_MODEL:]
        mix = g_a * y_a + g_b * y_b
        x = x + mix @ w_out[l].astype(x.dtype)

        h2 = rms_norm(x, norm_ffn_g[l])
        u = h2 @ w_up[l].astype(x.dtype)
        u = causal_depthwise_conv(u, ffn_dw_w[l], ffn_dw_b[l])
        ff = jax.nn.silu(u[..., :FFN_DIM]) * u[..., FFN_DIM:]
        x = x + ff @ w_down[l].astype(x.dtype)
    return rms_norm(x, norm_final_g)
```

```python
import os as _os
import numpy as np
import concourse.bass as bass
import concourse.mybir as mybir
from concourse.bass_utils import run_bass_kernel_spmd

F32 = mybir.dt.float32
BF16 = mybir.dt.bfloat16
ALU = mybir.AluOpType
AF = mybir.ActivationFunctionType

D = 1024
S = 4096
NB = 8
T = 256
NTILES = S // T
H = 4
DK = 256
DV = 512
CONV_K = 31
FFN = 3072
EPS = 1e-6
N_IN = 10240
NSLOT = 3

O_GMIX, O_GFFN, O_GATEB, O_CDWB, O_LNG, O_LNB, O_CPB = 0, 8, 16, 32, 40, 48, 56
O_CW = 64
O_FW = O_CW + 8 * 31
O_FB = O_FW + 48 * 3
NCOL = O_FB + 48


class Buf:
    __slots__ = ("w", "r")

    def __init__(self):
        self.w = None
        self.r = []


def bufs(n):
    return [Buf() for _ in range(n)]


class Sched:
    ENGS = ("pe", "act", "dve", "pool", "sp")

    def __init__(self, n_dma_sems=12):
        self.ops = []
        self.n_dma_sems = n_dma_sems
        self.dma_count = {}
        self.dma_last = {}
        self.stopped = False

    def add(self, eng, fn, reads=(), writes=(), dma=False, force=False):
        if self.stopped and not force:
            return None
        deps = set()
        for b in reads:
            if b.w is not None:
                deps.add(b.w)
        for b in writes:
            if b.w is not None:
                deps.add(b.w)
            deps.update(b.r)
        i = len(self.ops)
        if dma:
            k = self.dma_count.get(eng, 0)
            self.dma_count[eng] = k + 1
            slot = k % self.n_dma_sems
            dom = ("dma", eng, slot)
            prev = self.dma_last.get(dom)
            if prev is not None:
                deps.add(prev)
            self.dma_last[dom] = i
        else:
            dom = eng
        self.ops.append(dict(eng=eng, fn=fn, deps=sorted(deps), dma=dma, dom=dom))
        for b in reads:
            b.r.append(i)
        for b in writes:
            b.w = i
            b.r = []
        return i

    def finalize(self, nc):
        ops = self.ops
        know = {e: {} for e in self.ENGS}
        vcs = [None] * len(ops)
        waits = [None] * len(ops)
        needed = set()
        seqc = {}
        for i, op in enumerate(ops):
            E = op["eng"]
            dom = op["dom"]
            seq = seqc.get(dom, 0) + 1
            seqc[dom] = seq
            op["seq"] = seq
            K = know[E]
            w = []
            for d in sorted(op["deps"], reverse=True):
                dop = ops[d]
                if E == "pe" and dop["dom"] == "pe":
                    continue
                if K.get(dop["dom"], 0) >= dop["seq"]:
                    continue
                w.append(d)
                needed.add(d)
                for k2, v2 in vcs[d].items():
                    if K.get(k2, 0) < v2:
                        K[k2] = v2
            waits[i] = w
            v = dict(K)
            v[dom] = seq
            vcs[i] = v
        cnt = {}
        for i, op in enumerate(ops):
            if op["dma"] or i in needed:
                c = cnt.get(op["dom"], 0) + 1
                cnt[op["dom"]] = c
                op["val"] = c * (16 if op["dma"] else 1)
                op["inc"] = True
            else:
                op["inc"] = False
        sems = {}
        for dom in cnt:
            nm = dom if isinstance(dom, str) else "d_%s_%d" % (dom[1], dom[2])
            sems[dom] = nc.alloc_semaphore("s_" + nm)
        self.stats = dict(n_ops=len(ops), n_waits=sum(len(w) for w in waits),
                          n_inc=sum(1 for o in ops if o["inc"]), maxval=max(cnt.values()))
        per_eng = {e: [] for e in self.ENGS}
        for i, op in enumerate(ops):
            per_eng[op["eng"]].append(i)

        def emit(ename, e):
            for i in per_eng[ename]:
                op = ops[i]
                for d in waits[i]:
                    dop = ops[d]
                    e.wait_ge(sems[dop["dom"]], dop["val"])
                last = op["fn"](e)
                if op["inc"]:
                    assert last is not None
                    last.then_inc(sems[op["dom"]], 16 if op["dma"] else 1)

        with nc.Block() as block:
            @block.tensor
            def _(e):
                emit("pe", e)

            @block.scalar
            def _(e):
                emit("act", e)

            @block.vector
            def _(e):
                emit("dve", e)

            @block.gpsimd
            def _(e):
                emit("pool", e)

            @block.sync
            def _(e):
                emit("sp", e)


def build_program(ntiles=NTILES):
    nc = bass.Bass("TRN2", target_bir_lowering=False)
    sc = Sched()

    def din(name, shape):
        return nc.dram_tensor(name, list(shape), F32, kind="ExternalInput").ap()

    x_d = din("x", [S, D])
    w_in_d = din("w_in", [D, N_IN])
    w_cp_d = din("w_conv_proj", [D, D])
    w_rp_d = din("w_ret_proj", [2 * D, D])
    w_out_d = din("w_out", [D, D])
    w_up_d = din("w_up", [D, 2 * FFN])
    w_dn_d = din("w_down", [FFN, D])
    colpack_d = din("colpack", [128, NCOL])
    retg_d = din("ret_norm_g", [2 * D])
    fing_d = din("norm_final_g", [D])
    cos_d = din("cos_t", [128, S])
    sin_d = din("sin_t", [128, S])
    mask_d = din("mask_t", [128, 512])
    sd_d = din("sd_t", [128, 1024])
    cd_d = din("cd_t", [128, 1024])
    ident_d = din("ident", [128, 128])
    out_d = nc.dram_tensor("out", [S, D], F32, kind="ExternalOutput").ap()

    units = []
    for c in (0, 1024, 512, 1536):
        units.append((w_in_d, 0, c))
    for i in range(4):
        units.append((w_in_d, 0, 8192 + 512 * i))
    for i in range(2):
        units.append((w_in_d, 0, 2048 + 512 * i))
    for i in range(2):
        units.append((w_in_d, 0, 3072 + 512 * i))
    for i in range(4):
        units.append((w_in_d, 0, 4096 + 512 * i))
    for i in range(4):
        units.append((w_in_d, 0, 6144 + 512 * i))
    for i in range(2):
        units.append((w_cp_d, 0, 512 * i))
    for ch in range(2):
        for kg in range(2):
            units.append((w_rp_d, 1024 * kg, 512 * ch))
    for i in range(2):
        units.append((w_out_d, 0, 512 * i))
    for p in range(6):
        units.append((w_up_d, 0, 512 * p))
        units.append((w_up_d, 0, FFN + 512 * p))
    for kg in range(3):
        for hf in range(2):
            units.append((w_dn_d, 1024 * kg, 512 * hf))
    NU = len(units)
    assert NU == 46
    wsc = nc.dram_tensor("wsc", [NU, 128, 4096], BF16).ap()
    B_wsc = bufs(NU)

    def sb(name, shape, dt):
        return nc.alloc_sbuf_tensor(name, list(shape), dt)

    ring = [sb("ring%d" % i, [128, 8, 512], BF16) for i in range(NSLOT)]
    B_ring = bufs(NSLOT)
    xt = [sb("xt%d" % i, [128, 2, D], F32) for i in range(2)]
    B_x = [bufs(2), bufs(2)]
    xs = sb("xs", [128, 2, D], BF16)
    B_xs = bufs(2)
    junk = [sb("junk0", [128, D], BF16)] * 2
    B_junk = [Buf()] * 2
    jkc = [0]
    ubc = [0]
    hT = sb("hT", [128, 8, T], BF16)
    B_hT = bufs(8)
    a_t = sb("a_t", [128, 8, 30 + T], BF16)
    B_a = bufs(8)
    B_ah = Buf()
    gatesT = sb("gatesT", [128, 16, T], BF16)
    B_gates = bufs(16)
    qT = sb("qT", [128, 8, T], BF16)
    kT = sb("kT", [128, 8, T], BF16)
    qcT = sb("qcT", [128, 8, T], BF16)
    B_qT, B_kT, B_qcT = bufs(8), bufs(8), bufs(8)
    NTMP = 4
    tmpf = [sb("tmpf%d" % i, [128, 512], F32) for i in range(NTMP)]
    B_tmpf = bufs(NTMP)
    cs_t = [[sb("cs%d_%d" % (i, j), [128, T], F32) for j in range(2)] for i in range(2)]
    B_cs = bufs(2)
    vtok = sb("vtok", [128, 2, 2048], BF16)
    B_v = [bufs(4), bufs(4)]
    sg = sb("sg", [128, 2, 2048], BF16)
    B_sg = [bufs(4), bufs(4)]
    acf = sb("acf", [128, 8, T], BF16)
    sq = sb("sq", [128, 8, T], BF16)
    B_acf, B_sq = bufs(8), bufs(8)
    f8 = sb("f8", [128, 8, T], F32)
    B_f8 = bufs(8)
    anT = acf
    B_an = B_acf
    stt = sb("stt", [128, 4, T], F32)
    B_stt = bufs(4)
    ktok = [sb("ktok0", [128, 1024], BF16)] * 2
    B_ktok = [Buf()] * 2
    stm = [sb("stm%d" % i, [128, 512], BF16) for i in range(2)]
    B_stm = bufs(2)
    rn = [sb("rn%d" % i, [128, 512], BF16) for i in range(2)]
    B_rn = bufs(2)
    rg0 = sb("rg0", [128, 2048], BF16)
    rg = [rg0, rg0]
    B_rg0 = bufs(4)
    B_rg = [B_rg0, B_rg0]
    rgT = sb("rgT", [128, 16, T], BF16)
    B_rgT = [bufs(2), bufs(2)]
    R32 = sb("R32", [128, 4, 2, 512], F32)
    Rb = sb("Rb", [128, 4, 2, 512], BF16)
    B_R32 = [bufs(2) for _ in range(4)]
    B_Rb = [bufs(2) for _ in range(4)]
    mixT = qcT
    B_mix = B_qcT
    NUB = 4
    ub = [sb("ub%d" % i, [128, 2, 2 + T], BF16) for i in range(NUB)]
    B_ub = bufs(NUB)
    uhalo = sb("uhalo", [128, 48, 2], BF16)
    B_uh = bufs(48)
    ffT = sb("ffT", [128, 24, T], BF16)
    B_ff = bufs(24)
    NDG = 32
    dg = [sb("dg%d" % i, [128, 128], BF16) for i in range(NDG)]
    B_dg = bufs(NDG)
    small = sb("small", [128, 64], F32)
    smi = [0]
    colpack = sb("colpack_s", [128, NCOL], F32)
    identf = sb("identf", [128, 128], F32)
    identb = sb("identb", [128, 128], BF16)
    onesb = sb("onesb", [128, 128], BF16)
    maskD = sb("maskD", [128, 512], F32)
    sdC = sb("sdC", [128, 1024], BF16)
    cdC = sb("cdC", [128, 4, T], BF16)
    Gb = sb("Gb", [128, 2048], BF16)
    gfin = sb("gfin", [128, D], F32)
    negh = sb("negh", [128, T], F32)
    B_const = Buf()

    NPF = 6
    psum = [nc.alloc_psum_tensor("ps%d" % i, [128, 512], F32) for i in range(NPF)]
    psum_b = [None] * NPF + [nc.alloc_psum_tensor("pb%d" % i, [128, 1024], BF16) for i in range(2)]
    B_ps = bufs(8)
    psc = [0]
    psbc = [0]

    def ps_alloc():
        i = psc[0] % NPF
        psc[0] += 1
        return i

    def psb_alloc():
        i = NPF + psbc[0] % 2
        psbc[0] += 1
        return i

    def col(off):
        return colpack[:, off:off + 1]

    def seq(*fns):
        def f(e):
            r = None
            for g in fns:
                r = g(e)
            return r
        return f

    rr = {"n": 0}

    def I(method, *args, **kw):
        return lambda e: getattr(e, method)(*args, **kw)

    def dma_sp(out, in_, reads, writes):
        return sc.add("sp", I("dma_start", out=out, in_=in_), reads, writes, dma=True)

    dma_sp(colpack[:, :], colpack_d, [], [B_const])
    dma_sp(identf[:, :], ident_d, [], [B_const])
    dma_sp(maskD[:, :], mask_d, [], [B_const])
    R32f = R32[:, :, :, :].rearrange("p a b c -> p (a b c)")
    stages = [
        (xt[0][:, :, :].rearrange("p a b -> p (a b)"), list(B_x[0])),
        (xt[1][:, :, :].rearrange("p a b -> p (a b)"), list(B_x[1])),
        (f8[:, :, :].rearrange("p a b -> p (a b)"), list(B_f8)),
        (R32f[:, 0:2048], B_R32[0] + B_R32[1]),
        (R32f[:, 2048:4096], B_R32[2] + B_R32[3]),
    ]
    dma_sp(stages[0][0][:, 0:1024], sd_d, [], stages[0][1])
    sc.add("dve", I("tensor_copy", out=sdC[:, :], in_=stages[0][0][:, 0:1024]), stages[0][1], [B_const])
    dma_sp(stages[1][0][:, 0:1024], cd_d, [], stages[1][1])
    sc.add("dve", I("tensor_copy", out=cdC[:, :, :].rearrange("p h t -> p (h t)"), in_=stages[1][0][:, 0:1024]),
           stages[1][1], [B_const])
    dma_sp(stages[2][0], retg_d.partition_broadcast(128), [], stages[2][1])
    sc.add("dve", I("tensor_copy", out=Gb[:, :], in_=stages[2][0]), stages[2][1], [B_const])
    dma_sp(gfin[:, :], fing_d.partition_broadcast(128), [], [B_const])
    sc.add("dve", I("tensor_copy", out=identb[:, :], in_=identf[:, :]), [B_const], [B_const])
    sc.add("pool", I("memset", onesb[:, :], 1.0), [], [B_const])
    sc.add("pool", I("memset", negh[:, :], -0.5), [], [B_const])
    sc.add("pool", I("memset", a_t[:, :, 0:30], 0.0), [], [B_ah])
    sc.add("pool", I("memset", uhalo[:, :, :], 0.0), [], B_uh)
    sc.add("pool", I("memset", Rb[:, :, :, :], 0.0), [], [b for l in B_Rb for b in l])

    for u, (src, r0, c0) in enumerate(units):
        srcv = src[r0:r0 + 1024, c0:c0 + 512].rearrange("(k p) c -> p k c", p=128)
        dstv = wsc[u].rearrange("p (k c) -> p k c", k=8)
        slot = u % NSLOT
        for hf in range(2):
            sv, sB = stages[(2 * u + hf) % len(stages)]
            dma_sp(sv.rearrange("p (k c) -> p k c", k=4), srcv[:, 4 * hf:4 * hf + 4, :], [], sB)
            dstr = ring[slot][:, 4 * hf:4 * hf + 4, :].rearrange("p k c -> p (k c)")
            if hf == 0:
                sc.add("dve", I("tensor_copy", out=dstr, in_=sv), sB + [B_ring[slot]], [B_ring[slot]])
            else:
                sc.add("act", I("activation", out=dstr, in_=sv, func=AF.Copy), sB + [B_ring[slot]], [B_ring[slot]])
        sc.add("sp", I("dma_start", out=dstv, in_=ring[slot][:, :, :]), [B_ring[slot]], [B_wsc[u]], dma=True)
    sc.add("pool", I("memset", R32[:, :, :, :], 0.0), [], [b for l in B_R32 for b in l])

    ld = {"issued": 0, "used": 0}
    total_units = NU * ntiles

    def issue_load():
        n = ld["issued"]
        if n >= total_units:
            return
        slot = n % NSLOT
        u = n % NU
        srcv = wsc[u].rearrange("p (k c) -> p k c", k=8)
        sc.add("sp", I("dma_start", out=ring[slot][:, :, :], in_=srcv),
               [B_wsc[u]], [B_ring[slot]], dma=True)
        ld["issued"] = n + 1

    def next_unit():
        n = ld["used"]
        while ld["issued"] < min(n + NSLOT - 1, total_units):
            issue_load()
        ld["used"] = n + 1
        slot = n % NSLOT
        return ring[slot], B_ring[slot]

    def load_x(ti):
        par = ti % 2
        for blk in range(2):
            r0 = ti * T + blk * 128
            dma_sp(xt[par][:, blk, :], x_d[r0:r0 + 128, :], [], [B_x[par][blk]])
        dma_sp(cs_t[par][0][:, :], cos_d[:, ti * T:(ti + 1) * T], [], [B_cs[par]])
        dma_sp(cs_t[par][1][:, :], sin_d[:, ti * T:(ti + 1) * T], [], [B_cs[par]])

    def smallcol(n=1):
        if smi[0] + n > 64:
            smi[0] = 0
        i = smi[0]
        smi[0] += n
        return i

    B_small = bufs(64)

    def sm(c, n=1):
        return small[:, c:c + n]

    def rms_stats(par, blk):
        c_ss = smallcol()
        c_r = smallcol()
        jk = junk[jkc[0] % 2]
        Bjk = B_junk[jkc[0] % 2]
        jkc[0] += 1
        sc.add("act", I("activation", out=jk[:, :], in_=xt[par][:, blk, :], func=AF.Square,
                        accum_out=sm(c_ss)),
               [B_x[par][blk]], [Bjk, B_small[c_ss]])
        sc.add("dve", I("tensor_scalar", out=sm(c_r), in0=sm(c_ss), scalar1=1.0 / D, scalar2=EPS,
                        op0=ALU.mult, op1=ALU.add),
               [B_small[c_ss]], [B_small[c_r]])
        sc.add("pool", I("tensor_tensor", out=sm(c_r), in0=sm(c_r), in1=negh[:, 0:1], op=ALU.pow),
               [B_small[c_r], B_const], [B_small[c_r]])
        return c_r

    KSUB = _os.environ.get('KSUB', '')

    def ck2(name):
        if KSUB == name:
            sc.stopped = True

    def norm_to_hT(par, goff):
        for blk in range(2):
            c_r = rms_stats(par, blk)
            ck2("stats")
            sc.add("act", I("activation", out=xs[:, blk, :], in_=xt[par][:, blk, :], func=AF.Identity,
                            scale=sm(c_r)),
                   [B_x[par][blk], B_small[c_r]], [B_xs[blk]])
        ck2("xs")
        for half in range(2):
            pi = psb_alloc()
            pv = psum_b[pi][:, :].rearrange("p (k b t) -> p k b t", k=4, b=2)
            fns = []
            for kl in range(4):
                kc = half * 4 + kl
                for blk in range(2):
                    fns.append(I("transpose", out=pv[:, kl, blk, :],
                                 in_=xs[:, blk, kc * 128:(kc + 1) * 128], identity=identb[:, :]))
            sc.add("pe", seq(*fns), [B_xs[0], B_xs[1], B_const], [B_ps[pi]])
            ck2("tr")
            for kl in range(4):
                kc = half * 4 + kl
                src = pv[:, kl, :, :].rearrange("p b t -> p (b t)")
                kev = _os.environ.get('KEVAC', 'dve')
                if kev == 'none':
                    continue
                if (kl % 2 == 0 and kev == 'both') or kev == 'act':
                    sc.add("act", I("activation", out=hT[:, kc, :], in_=src, func=AF.Identity,
                                    scale=col(goff + kc)),
                           [B_ps[pi], B_const], [B_hT[kc]])
                else:
                    sc.add("dve", I("tensor_scalar", out=hT[:, kc, :], in0=src, scalar1=col(goff + kc),
                                    scalar2=None, op0=ALU.mult),
                           [B_ps[pi], B_const], [B_hT[kc]])

    def fm_group(unit, ubuf, lc0, nch, rhs_t, rhs_bufs, nk=8, pi=None, start=True, stop=True, k0=0):
        if pi is None:
            pi = ps_alloc()
        pv = psum[pi][:, :].rearrange("p (j t) -> p j t", j=2)
        fns = []
        for j in range(nch):
            for kc in range(nk):
                fns.append(I("matmul", pv[:, j, :], unit[:, kc, (lc0 + j) * 128:(lc0 + j + 1) * 128],
                             rhs_t[:, k0 + kc, :], start=(start and kc == 0), stop=(stop and kc == nk - 1)))
        sc.add("pe", seq(*fns), [ubuf] + list(rhs_bufs), [B_ps[pi]])
        return pi, pv

    def tmp_alloc():
        i = rr["n"] % NTMP
        rr["n"] += 1
        return i

    def v2(t_):
        return t_[:, :].rearrange("p (j t) -> p j t", j=2)

    dgc = [0]

    def build_diag(coff):
        i = dgc[0] % NDG
        dgc[0] += 1
        if dgc[0] % 2 == 0:
            sc.add("act", I("activation", out=dg[i][:, :], in_=identf[:, :], func=AF.Identity, scale=col(coff)),
                   [B_const], [B_dg[i]])
        else:
            sc.add("dve", I("tensor_scalar", out=dg[i][:, :], in0=identf[:, :], scalar1=col(coff),
                            scalar2=None, op0=ALU.mult),
                   [B_const], [B_dg[i]])
        return i

    load_x(0)
    out_dma_ops = []
    STOP = int(_os.environ.get('KSTOP', '99'))

    def ck(k):
        if STOP == k:
            sc.stopped = True

    KDBG = _os.environ.get('KDBG', '') == '1'

    def dbg(name, t_, shape, dt, rbufs):
        if not KDBG or ti != 0:
            return
        dd = nc.dram_tensor("dbg_" + name, list(shape), dt, kind="ExternalOutput").ap()
        o_ = sc.add("sp", I("dma_start", out=dd, in_=t_), list(rbufs), [], dma=True)
        out_dma_ops.append(o_)

    for ti in range(ntiles):
        par = ti % 2
        cosT, sinT = cs_t[par][0], cs_t[par][1]
        cb = cosT[:, :].unsqueeze(1).to_broadcast([128, 2, T])
        sbb = sinT[:, :].unsqueeze(1).to_broadcast([128, 2, T])

        ck(0)
        norm_to_hT(par, O_GMIX)

        dbg('hT', hT[:, :, :], [128, 8, T], BF16, B_hT)
        ck(1)
        for up in range(2):
            uA, bA = next_unit()
            uG, bG = next_unit()
            for q2 in range(2):
                m = up * 4 + q2 * 2
                pa, pva = fm_group(uA, bA, q2 * 2, 2, hT, B_hT)
                pg, pvg = fm_group(uG, bG, q2 * 2, 2, hT, B_hT)
                ts = tmp_alloc()
                sc.add("act", I("activation", out=tmpf[ts][:, :], in_=psum[pg][:, :], func=AF.Sigmoid),
                       [B_ps[pg]], [B_tmpf[ts]])
                sc.add("dve", I("tensor_tensor", out=a_t[:, m:m + 2, 30:30 + T], in0=pva,
                                in1=v2(tmpf[ts]), op=ALU.mult),
                       [B_ps[pa], B_tmpf[ts]], [B_a[m], B_a[m + 1]])

        dbg('a_t', a_t[:, :, :], [128, 8, 30 + T], BF16, B_a)
        ck(2)
        for ug in range(4):
            uu, bu = next_unit()
            for q2 in range(2):
                m = ug * 4 + q2 * 2
                pi, pv = fm_group(uu, bu, q2 * 2, 2, hT, B_hT)
                for j in range(2):
                    sc.add("act", I("activation", out=gatesT[:, m + j, :], in_=pv[:, j, :], func=AF.Sigmoid,
                                    bias=col(O_GATEB + m + j)),
                           [B_ps[pi], B_const], [B_gates[m + j]])

        dbg('gatesT', gatesT[:, :, :], [128, 16, T], BF16, B_gates)
        ck(3)
        for which in range(2):
            dstT, dstB = (qT, B_qT) if which == 0 else (kT, B_kT)
            for u2 in range(2):
                uu, bu = next_unit()
                for q2 in range(2):
                    hh = u2 * 2 + q2
                    pi, pv = fm_group(uu, bu, q2 * 2, 2, hT, B_hT)
                    tA, tB = tmp_alloc(), tmp_alloc()
                    tAv, tBv = v2(tmpf[tA]), v2(tmpf[tB])
                    sc.add("dve", I("tensor_tensor", out=tAv, in0=pv, in1=cb, op=ALU.mult),
                           [B_ps[pi], B_cs[par]], [B_tmpf[tA]])
                    sc.add("dve", I("tensor_tensor", out=tBv, in0=pv, in1=sbb, op=ALU.mult),
                           [B_ps[pi], B_cs[par]], [B_tmpf[tB]])
                    sc.add("dve", I("tensor_tensor", out=dstT[:, 2 * hh, :], in0=tAv[:, 0, :], in1=tBv[:, 1, :],
                                    op=ALU.subtract),
                           [B_tmpf[tA], B_tmpf[tB]], [dstB[2 * hh]])
                    sc.add("dve", I("tensor_tensor", out=dstT[:, 2 * hh + 1, :], in0=tAv[:, 1, :], in1=tBv[:, 0, :],
                                    op=ALU.add),
                           [B_tmpf[tA], B_tmpf[tB]], [dstB[2 * hh + 1]])
        for hh in range(4):
            sc.add("pool", I("tensor_tensor", out=qcT[:, 2 * hh:2 * hh + 2, :], in0=qT[:, 2 * hh:2 * hh + 2, :],
                             in1=cdC[:, hh, :].unsqueeze(1).to_broadcast([128, 2, T]), op=ALU.mult),
                   [B_qT[2 * hh], B_qT[2 * hh + 1], B_const], [B_qcT[2 * hh], B_qcT[2 * hh + 1]])

        dbg('qT', qT[:, :, :], [128, 8, T], BF16, B_qT)
        dbg('kT', kT[:, :, :], [128, 8, T], BF16, B_kT)
        dbg('qcT', qcT[:, :, :], [128, 8, T], BF16, B_qcT)
        ck(4)
        for which in range(2):
            for u4 in range(4):
                uu, bu = next_unit()
                for blk in range(2):
                    pi = ps_alloc()
                    fns = []
                    for kc in range(8):
                        fns.append(I("matmul", psum[pi][:, :], hT[:, kc, blk * 128:(blk + 1) * 128], uu[:, kc, :],
                                     start=(kc == 0), stop=(kc == 7)))
                    sc.add("pe", seq(*fns), [bu] + B_hT, [B_ps[pi]])
                    dst_v = vtok[:, blk, u4 * 512:(u4 + 1) * 512]
                    dst_g = sg[:, blk, u4 * 512:(u4 + 1) * 512]
                    if which == 0:
                        if blk == 0:
                            sc.add("act", I("activation", out=dst_v, in_=psum[pi][:, :], func=AF.Copy),
                                   [B_ps[pi]], [B_v[blk][u4]])
                        else:
                            sc.add("dve", I("tensor_copy", out=dst_v, in_=psum[pi][:, :]),
                                   [B_ps[pi]], [B_v[blk][u4]])
                    else:
                        sc.add("act", I("activation", out=dst_g, in_=psum[pi][:, :], func=AF.Silu),
                               [B_ps[pi]], [B_sg[blk][u4]])
                        sc.add("pool", I("tensor_tensor", out=dst_g, in0=dst_g,
                                         in1=Gb[:, u4 * 512:(u4 + 1) * 512], op=ALU.mult),
                               [B_sg[blk][u4], B_const], [B_sg[blk][u4]])

        dbg('vtok', vtok[:, :, :], [128, 2, 2048], BF16, B_v[0] + B_v[1])
        dbg('sg', sg[:, :, :], [128, 2, 2048], BF16, B_sg[0] + B_sg[1])
        ck(5)
        ps1, ps2 = ps_alloc(), ps_alloc()
        for kp in range(4):
            pi = ps_alloc()
            pv = v2(psum[pi])
            for j in range(2):
                kc = kp * 2 + j
                dgs = [build_diag(O_CW + kc * 31 + tap) for tap in range(CONV_K)]
                fns = []
                for tap in range(CONV_K):
                    fns.append(I("matmul", pv[:, j, :], dg[dgs[tap]][:, :], a_t[:, kc, tap:tap + T],
                                 start=(tap == 0), stop=(tap == CONV_K - 1)))
                sc.add("pe", seq(*fns), [B_a[kc], B_ah] + [B_dg[i] for i in dgs], [B_ps[pi]])
            for j in range(2):
                kc = kp * 2 + j
                sc.add("act", I("activation", out=acf[:, kc, :], in_=pv[:, j, :], func=AF.Identity,
                                bias=col(O_CDWB + kc)),
                       [B_ps[pi], B_const], [B_acf[kc]])
                sc.add("act", I("activation", out=sq[:, kc, :], in_=pv[:, j, :], func=AF.Square,
                                bias=col(O_CDWB + kc)),
                       [B_ps[pi], B_const], [B_sq[kc]])
        sc.add("pool", I("tensor_copy", out=a_t[:, :, 0:30], in_=a_t[:, :, T:T + 30]), B_a, [B_ah])
        fns = [I("matmul", psum[ps1][:, 0:T], onesb[:, :], acf[:, kc, :], start=(kc == 0), stop=(kc == 7))
               for kc in range(8)]
        sc.add("pe", seq(*fns), B_acf + [B_const], [B_ps[ps1]])
        fns = [I("matmul", psum[ps2][:, 0:T], onesb[:, :], sq[:, kc, :], start=(kc == 0), stop=(kc == 7))
               for kc in range(8)]
        sc.add("pe", seq(*fns), B_sq + [B_const], [B_ps[ps2]])
        mean_t, var_t, nmr_t, tmp_t = stt[:, 0, :], stt[:, 1, :], stt[:, 2, :], stt[:, 3, :]
        sc.add("dve", I("tensor_scalar", out=mean_t, in0=psum[ps1][:, 0:T], scalar1=1.0 / D, scalar2=None,
                        op0=ALU.mult), [B_ps[ps1]], [B_stt[0]])
        sc.add("dve", I("tensor_tensor", out=tmp_t, in0=mean_t, in1=mean_t, op=ALU.mult),
               [B_stt[0]], [B_stt[3]])
        sc.add("dve", I("scalar_tensor_tensor", out=var_t, in0=psum[ps2][:, 0:T], scalar=1.0 / D, in1=tmp_t,
                        op0=ALU.mult, op1=ALU.subtract),
               [B_ps[ps2], B_stt[3]], [B_stt[1]])
        sc.add("dve", I("tensor_scalar", out=var_t, in0=var_t, scalar1=EPS, scalar2=None,
                        op0=ALU.add), [B_stt[1]], [B_stt[1]])
        sc.add("pool", I("tensor_tensor", out=var_t, in0=var_t, in1=negh[:, :], op=ALU.pow),
               [B_stt[1], B_const], [B_stt[1]])
        sc.add("dve", I("scalar_tensor_tensor", out=nmr_t, in0=mean_t, scalar=-1.0, in1=var_t,
                        op0=ALU.mult, op1=ALU.mult),
               [B_stt[0], B_stt[1]], [B_stt[2]])
        sc.add("dve", I("tensor_tensor", out=f8[:, :, :], in0=acf[:, :, :],
                        in1=var_t.unsqueeze(1).to_broadcast([128, 8, T]), op=ALU.mult),
               B_acf + [B_stt[1]], B_f8)
        sc.add("pool", I("tensor_tensor", out=f8[:, :, :], in0=f8[:, :, :],
                         in1=nmr_t.unsqueeze(1).to_broadcast([128, 8, T]), op=ALU.add),
               B_f8 + [B_stt[2]], B_f8)
        for kc in range(8):
            sc.add("act", I("activation", out=anT[:, kc, :], in_=f8[:, kc, :], func=AF.Silu,
                            scale=col(O_LNG + kc), bias=col(O_LNB + kc)),
                   [B_f8[kc], B_const], [B_an[kc]])

        dbg('anT', anT[:, :, :], [128, 8, T], BF16, B_an)
        ck(6)
        for blk in range(2):
            tsl = slice(blk * 128, (blk + 1) * 128)
            kb = blk
            pk = psb_alloc()
            fns = [I("transpose", out=psum_b[pk][:, kcc * 128:(kcc + 1) * 128], in_=kT[:, kcc, tsl],
                     identity=identb[:, :]) for kcc in range(8)]
            sc.add("pe", seq(*fns), B_kT + [B_const], [B_ps[pk]])
            sc.add("dve", I("tensor_tensor", out=ktok[kb][:, :], in0=psum_b[pk][:, :], in1=sdC[:, :], op=ALU.mult),
                   [B_ps[pk], B_const], [B_ktok[kb]])
            pS = ps_alloc()
            fns = []
            for hh in range(4):
                for c2 in range(2):
                    fns.append(I("matmul", psum[pS][:, hh * 128:(hh + 1) * 128], kT[:, 2 * hh + c2, tsl],
                                 qT[:, 2 * hh + c2, tsl], start=(c2 == 0), stop=(c2 == 1)))
            sc.add("pe", seq(*fns), B_kT + B_qT, [B_ps[pS]])
            sc.add("dve", I("tensor_tensor", out=stm[kb][:, :], in0=psum[pS][:, :], in1=maskD[:, :], op=ALU.mult),
                   [B_ps[pS], B_const], [B_stm[kb]])
            for hh in range(4):
                po = ps_alloc()
                vh = vtok[:, blk, hh * 512:(hh + 1) * 512]
                fns = [I("matmul", psum[po][:, :], stm[kb][:, hh * 128:(hh + 1) * 128], vh, start=True, stop=False)]
                for c2 in range(2):
                    fns.append(I("matmul", psum[po][:, :], qcT[:, 2 * hh + c2, tsl], Rb[:, hh, c2, :],
                                 start=False, stop=(c2 == 1)))
                sc.add("pe", seq(*fns),
                       [B_stm[kb], B_v[blk][hh], B_qcT[2 * hh], B_qcT[2 * hh + 1], B_Rb[hh][0], B_Rb[hh][1]],
                       [B_ps[po]])
                c_st = smallcol(6)
                c_mv = smallcol(2)
                c_rs = smallcol()
                c_nb = smallcol()
                Bst = [B_small[c_st + i] for i in range(6)]
                Bmv = [B_small[c_mv], B_small[c_mv + 1]]
                sc.add("dve", I("bn_stats", out=sm(c_st, 6), in_=psum[po][:, :]), [B_ps[po]], Bst)
                sc.add("dve", I("bn_aggr", out=sm(c_mv, 2), in_=sm(c_st, 6)), Bst, Bmv)
                sc.add("dve", I("tensor_scalar", out=sm(c_rs), in0=sm(c_mv + 1), scalar1=EPS, scalar2=None,
                                op0=ALU.add), Bmv, [B_small[c_rs]])
                sc.add("pool", I("tensor_tensor", out=sm(c_rs), in0=sm(c_rs), in1=negh[:, 0:1], op=ALU.pow),
                       [B_small[c_rs], B_const], [B_small[c_rs]])
                sc.add("dve", I("scalar_tensor_tensor", out=sm(c_nb), in0=sm(c_mv), scalar=-1.0, in1=sm(c_rs),
                                op0=ALU.mult, op1=ALU.mult),
                       Bmv + [B_small[c_rs]], [B_small[c_nb]])
                ri = (blk * 4 + hh) % 2
                sc.add("act", I("activation", out=rn[ri][:, :], in_=psum[po][:, :], func=AF.Identity,
                                scale=sm(c_rs), bias=sm(c_nb)),
                       [B_ps[po], B_small[c_rs], B_small[c_nb]], [B_rn[ri]])
                sc.add("pool", I("tensor_tensor", out=rg[blk][:, hh * 512:(hh + 1) * 512], in0=rn[ri][:, :],
                                 in1=sg[:, blk, hh * 512:(hh + 1) * 512], op=ALU.mult),
                       [B_rn[ri], B_sg[blk][hh]], [B_rg[blk][hh]])
            for hh in range(4):
                gC = float(GAMMA[hh] ** 128)
                vh = vtok[:, blk, hh * 512:(hh + 1) * 512]
                for c2 in range(2):
                    pr = ps_alloc()
                    kcc = 2 * hh + c2
                    sc.add("pe", I("matmul", psum[pr][:, :], ktok[kb][:, kcc * 128:(kcc + 1) * 128], vh,
                                   start=True, stop=True),
                           [B_ktok[kb], B_v[blk][hh]], [B_ps[pr]])
                    sc.add("dve", I("scalar_tensor_tensor", out=R32[:, hh, c2, :], in0=R32[:, hh, c2, :], scalar=gC,
                                    in1=psum[pr][:, :], op0=ALU.mult, op1=ALU.add),
                           [B_ps[pr], B_R32[hh][c2]], [B_R32[hh][c2]])
                    sc.add("pool", I("tensor_copy", out=Rb[:, hh, c2, :], in_=R32[:, hh, c2, :]),
                           [B_R32[hh][c2]], [B_Rb[hh][c2]])
            for fh in range(2):
                pt = psb_alloc()
                fns = []
                for fl in range(8):
                    f = fh * 8 + fl
                    fns.append(I("transpose", out=psum_b[pt][:, fl * 128:(fl + 1) * 128],
                                 in_=rg[blk][:, f * 128:(f + 1) * 128], identity=identb[:, :]))
                sc.add("pe", seq(*fns), [B_rg[blk][2 * fh], B_rg[blk][2 * fh + 1], B_const], [B_ps[pt]])
                srcv = psum_b[pt][:, :].rearrange("p (f t) -> p f t", f=8)
                dstv = rgT[:, fh * 8:(fh + 1) * 8, tsl]
                sc.add("dve", I("tensor_copy", out=dstv, in_=srcv), [B_ps[pt]], [B_rgT[blk][fh]])

        dbg('rgT', rgT[:, :, :], [128, 16, T], BF16, [b for l in B_rgT for b in l])
        dbg('R32', R32[:, :, :, :], [128, 4, 2, 512], F32, [b for l in B_R32 for b in l])
        ck(7)
        for u2 in range(2):
            uu, bu = next_unit()
            for q2 in range(2):
                m = u2 * 4 + q2 * 2
                pi, pv = fm_group(uu, bu, q2 * 2, 2, anT, B_an)
                for j in range(2):
                    sc.add("dve", I("scalar_tensor_tensor", out=f8[:, m + j, :], in0=pv[:, j, :],
                                    scalar=col(O_CPB + m + j), in1=gatesT[:, m + j, :],
                                    op0=ALU.add, op1=ALU.mult),
                           [B_ps[pi], B_const, B_gates[m + j]], [B_f8[m + j]])

        ck(8)
        all_rgT = [b for l in B_rgT for b in l]
        for ch in range(2):
            u0, b0 = next_unit()
            u1, b1 = next_unit()
            for q2 in range(2):
                pi = ps_alloc()
                pv = v2(psum[pi])
                fns = []
                for j in range(2):
                    lc = q2 * 2 + j
                    for (un, k0) in ((u0, 0), (u1, 8)):
                        for kc in range(8):
                            fns.append(I("matmul", pv[:, j, :], un[:, kc, lc * 128:(lc + 1) * 128],
                                         rgT[:, k0 + kc, :], start=(k0 == 0 and kc == 0),
                                         stop=(k0 == 8 and kc == 7)))
                sc.add("pe", seq(*fns), [b0, b1] + all_rgT, [B_ps[pi]])
                m = ch * 4 + q2 * 2
                ts = tmp_alloc()
                tv = v2(tmpf[ts])
                sc.add("dve", I("tensor_tensor", out=tv, in0=pv, in1=gatesT[:, 8 + m:8 + m + 2, :], op=ALU.mult),
                       [B_ps[pi], B_gates[8 + m], B_gates[9 + m]], [B_tmpf[ts]])
                sc.add("pool", I("tensor_tensor", out=mixT[:, m:m + 2, :], in0=tv, in1=f8[:, m:m + 2, :], op=ALU.add),
                       [B_tmpf[ts], B_f8[m], B_f8[m + 1]], [B_mix[m], B_mix[m + 1]])

        dbg('mixT', mixT[:, :, :], [128, 8, T], BF16, B_mix)
        ck(9)
        for hf in range(2):
            uu, bu = next_unit()
            for blk in range(2):
                pi = ps_alloc()
                fns = [I("matmul", psum[pi][:, :], mixT[:, kc, blk * 128:(blk + 1) * 128], uu[:, kc, :],
                         start=(kc == 0), stop=(kc == 7)) for kc in range(8)]
                sc.add("pe", seq(*fns), [bu] + B_mix, [B_ps[pi]])
                xv = xt[par][:, blk, hf * 512:(hf + 1) * 512]
                sc.add("dve", I("tensor_tensor", out=xv, in0=xv, in1=psum[pi][:, :], op=ALU.add),
                       [B_ps[pi], B_x[par][blk]], [B_x[par][blk]])

        if ti + 1 < ntiles:
            load_x(ti + 1)

        dbg('x1', xt[par][:, :, :], [128, 2, D], F32, B_x[par])
        ck(10)
        norm_to_hT(par, O_GFFN)

        ck(11)
        for p in range(6):
            uA, bA = next_unit()
            uB, bB = next_unit()
            for q2 in range(2):
                fa = 4 * p + 2 * q2
                res = []
                for (uu, bu, f0) in ((uA, bA, fa), (uB, bB, 24 + fa)):
                    pi, pv = fm_group(uu, bu, q2 * 2, 2, hT, B_hT)
                    bi = ubc[0] % NUB
                    ubc[0] += 1
                    sc.add("pool", I("tensor_copy", out=ub[bi][:, :, 0:2], in_=uhalo[:, f0:f0 + 2, :]),
                           [B_uh[f0], B_uh[f0 + 1]], [B_ub[bi]])
                    sc.add("act", I("activation", out=ub[bi][:, :, 2:2 + T], in_=pv, func=AF.Copy),
                           [B_ps[pi]], [B_ub[bi]])
                    sc.add("pool", I("tensor_copy", out=uhalo[:, f0:f0 + 2, :], in_=ub[bi][:, :, T:T + 2]),
                           [B_ub[bi]], [B_uh[f0], B_uh[f0 + 1]])
                    pc = ps_alloc()
                    pcv = v2(psum[pc])
                    for j in range(2):
                        f = f0 + j
                        dgs = [build_diag(O_FW + f * 3 + tap) for tap in range(3)]
                        fns = [I("matmul", pcv[:, j, :], dg[dgs[tap]][:, :], ub[bi][:, j, tap:tap + T],
                                 start=(tap == 0), stop=(tap == 2)) for tap in range(3)]
                        sc.add("pe", seq(*fns), [B_ub[bi]] + [B_dg[i] for i in dgs], [B_ps[pc]])
                    res.append((pc, pcv))
                (pcA, pcvA), (pcB, pcvB) = res
                ts = tmp_alloc()
                tv = v2(tmpf[ts])
                for j in range(2):
                    sc.add("act", I("activation", out=tv[:, j, :], in_=pcvA[:, j, :], func=AF.Silu,
                                    bias=col(O_FB + fa + j)),
                           [B_ps[pcA], B_const], [B_tmpf[ts]])
                for j in range(2):
                    sc.add("dve", I("scalar_tensor_tensor", out=ffT[:, fa + j, :], in0=pcvB[:, j, :],
                                    scalar=col(O_FB + 24 + fa + j), in1=tv[:, j, :], op0=ALU.add, op1=ALU.mult),
                           [B_ps[pcB], B_const, B_tmpf[ts]], [B_ff[fa + j]])

        dbg('ffT', ffT[:, :, :], [128, 24, T], BF16, B_ff)
        ck(12)
        pd = [[ps_alloc() for _ in range(2)] for _ in range(2)]
        for kg in range(3):
            for hf in range(2):
                uu, bu = next_unit()
                for blk in range(2):
                    pi = pd[blk][hf]
                    fns = [I("matmul", psum[pi][:, :], ffT[:, kg * 8 + kc, blk * 128:(blk + 1) * 128], uu[:, kc, :],
                             start=(kg == 0 and kc == 0), stop=(kg == 2 and kc == 7)) for kc in range(8)]
                    sc.add("pe", seq(*fns), [bu] + B_ff[kg * 8:(kg + 1) * 8], [B_ps[pi]])
        for blk in range(2):
            for hf in range(2):
                pi = pd[blk][hf]
                xv = xt[par][:, blk, hf * 512:(hf + 1) * 512]
                sc.add("dve", I("tensor_tensor", out=xv, in0=xv, in1=psum[pi][:, :], op=ALU.add),
                       [B_ps[pi], B_x[par][blk]], [B_x[par][blk]])

        dbg('x2', xt[par][:, :, :], [128, 2, D], F32, B_x[par])
        ck(13)
        for blk in range(2):
            c_r = rms_stats(par, blk)
            sc.add("dve", I("scalar_tensor_tensor", out=xt[par][:, blk, :], in0=xt[par][:, blk, :], scalar=sm(c_r),
                            in1=gfin[:, :], op0=ALU.mult, op1=ALU.mult),
                   [B_x[par][blk], B_small[c_r], B_const], [B_x[par][blk]])
            r0 = ti * T + blk * 128
            oid = sc.add("pool", I("dma_start", out=out_d[r0:r0 + 128, :], in_=xt[par][:, blk, :]),
                         [B_x[par][blk]], [], dma=True)
            out_dma_ops.append(oid)

    fin = sc.add("pool", lambda e: None, [], [], force=True)
    sc.ops[fin]["deps"] = sorted(set(o for o in out_dma_ops if o is not None))
    sc.finalize(nc)
    return nc, sc


GAMMA = [1.0 - 2.0 ** (-5.0 - h) for h in range(H)]


def host_constants():
    half = DK // 2
    inv_freq = (np.float32(10000.0) ** (-np.arange(half, dtype=np.float32) / np.float32(half))).astype(np.float32)
    pos = np.arange(S, dtype=np.float32)
    ang = (inv_freq[:, None] * pos[None, :]).astype(np.float32)
    cos_t = np.cos(ang).astype(np.float32)
    sin_t = np.sin(ang).astype(np.float32)
    idx = np.arange(128, dtype=np.float64)
    mask = np.zeros((128, H, 128), np.float64)
    sd = np.zeros((128, H), np.float64)
    cd = np.zeros((H, 128), np.float64)
    for h in range(H):
        lg = np.log1p(-2.0 ** (-5.0 - h))
        diff = idx[None, :] - idx[:, None]
        m = np.where(diff >= 0, np.exp(np.where(diff >= 0, diff, 0.0) * lg), 0.0)
        mask[:, h, :] = m / 16.0
        sd[:, h] = np.exp((127.0 - idx) * lg) / 16.0
        cd[h, :] = np.exp((idx + 1.0) * lg)
    sd_t = np.repeat(sd, 256, axis=1)
    cd_t = np.broadcast_to(np.tile(cd, (1, 2)).reshape(1, H * 256), (128, H * 256))
    return dict(
        cos_t=np.ascontiguousarray(cos_t), sin_t=np.ascontiguousarray(sin_t),
        mask_t=np.ascontiguousarray(mask.reshape(128, 512).astype(np.float32)),
        sd_t=np.ascontiguousarray(sd_t.astype(np.float32)),
        cd_t=np.ascontiguousarray(cd_t.astype(np.float32)),
        ident=np.eye(128, dtype=np.float32),
    )


def make_colpack(inp):
    def colv(v, n):
        return np.asarray(v, np.float32).reshape(n, 128).T
    cp = np.zeros((128, NCOL), np.float32)
    cp[:, O_GMIX:O_GMIX + 8] = colv(inp["norm_mix_g"][0], 8)
    cp[:, O_GFFN:O_GFFN + 8] = colv(inp["norm_ffn_g"][0], 8)
    cp[:, O_GATEB:O_GATEB + 16] = colv(inp["gate_b"][0], 16)
    cp[:, O_CDWB:O_CDWB + 8] = colv(inp["conv_dw_b"][0], 8)
    cp[:, O_LNG:O_LNG + 8] = colv(inp["conv_ln_g"][0], 8)
    cp[:, O_LNB:O_LNB + 8] = colv(inp["conv_ln_b"][0], 8)
    cp[:, O_CPB:O_CPB + 8] = colv(inp["conv_proj_b"][0], 8)
    cw = np.asarray(inp["conv_dw_w"][0], np.float32).reshape(31, 8, 128)
    cp[:, O_CW:O_CW + 248] = cw.transpose(2, 1, 0).reshape(128, 248)
    fw = np.asarray(inp["ffn_dw_w"][0], np.float32).reshape(3, 48, 128)
    cp[:, O_FW:O_FW + 144] = fw.transpose(2, 1, 0).reshape(128, 144)
    cp[:, O_FB:O_FB + 48] = colv(inp["ffn_dw_b"][0], 48)
    return cp


_CACHE = {}
_LAST = None


def kernel(**inputs):
    ntiles = int(inputs.pop("_ntiles", NTILES))
    inp = {k: np.asarray(v) for k, v in inputs.items()}
    x = np.ascontiguousarray(inp["x"], dtype=np.float32)
    if ntiles not in _CACHE:
        _CACHE[ntiles] = build_program(ntiles)
    nc, sc = _CACHE[ntiles]
    consts = host_constants()
    shared = dict(
        w_in=np.ascontiguousarray(inp["w_in"][0], dtype=np.float32),
        w_conv_proj=np.ascontiguousarray(inp["w_conv_proj"][0], dtype=np.float32),
        w_ret_proj=np.ascontiguousarray(inp["w_ret_proj"][0], dtype=np.float32),
        w_out=np.ascontiguousarray(inp["w_out"][0], dtype=np.float32),
        w_up=np.ascontiguousarray(inp["w_up"][0], dtype=np.float32),
        w_down=np.ascontiguousarray(inp["w_down"][0], dtype=np.float32),
        colpack=make_colpack(inp),
        ret_norm_g=np.ascontiguousarray(inp["ret_norm_g"][0], dtype=np.float32),
        norm_final_g=np.ascontiguousarray(inp["norm_final_g"], dtype=np.float32),
        **consts,
    )
    in_maps = []
    ncores = int(_os.environ.get('KNCORES', NB))
    for b in range(ncores):
        m = dict(shared)
        m["x"] = x[b]
        in_maps.append(m)
    res = run_bass_kernel_spmd(nc, in_maps, core_ids=list(range(ncores)))
    global _LAST
    _LAST = res.results
    out = np.stack([np.asarray(r["out"], dtype=np.float32) for r in res.results], axis=0)
    return out
```
